# Optimizing a Trainium2 kernel written in Bass

```python
import jax, jax.numpy as jnp
from jax import lax
import numpy as np

D_MODEL = 1024
BATCH = 2
SEQ = 8192
DEPTH = 4

GRID_W = 64
CTX_LEN = 256
CHUNK = 128
Q_BLOCK = 128
EPS = 1e-6
ROPE_THETA = 10000.0
N_MOD = 6

A_HEADS = 4
A_HEAD_DIM = 64
A_WIDTH = A_HEADS * A_HEAD_DIM
ATT_Q_HEADS = 8
ATT_KV_HEADS = 2
ATT_HEAD_DIM = 64
ATT_REP = ATT_Q_HEADS // ATT_KV_HEADS
ATT_WIDTH = ATT_Q_HEADS * ATT_HEAD_DIM
ATT_KV_WIDTH = ATT_KV_HEADS * ATT_HEAD_DIM
ATT_SCALE = ATT_HEAD_DIM ** -0.5
ROPE_AXIS_DIM = ATT_HEAD_DIM // 2
ROPE_AXIS_FREQS = ROPE_AXIS_DIM // 2
C_WIDTH = 256
D_MIX = A_WIDTH + ATT_WIDTH + C_WIDTH
D_FF = 2816

OFF_AU = 0
OFF_AV = OFF_AU + A_WIDTH
OFF_Q = OFF_AV + A_WIDTH
OFF_K = OFF_Q + ATT_WIDTH
OFF_V = OFF_K + ATT_KV_WIDTH
OFF_CB = OFF_V + ATT_KV_WIDTH
OFF_CC = OFF_CB + C_WIDTH
OFF_CH = OFF_CC + C_WIDTH
D_IN = OFF_CH + C_WIDTH

kernel_name = "hybrid_parallel_heads_diffusion_trunk"


def rms_norm(x, g):
    xf = x.astype(jnp.float32)
    y = xf * lax.rsqrt(jnp.mean(xf * xf, axis=-1, keepdims=True) + EPS)
    return (y * g.astype(jnp.float32)).astype(x.dtype)


def layer_norm_plain(x):
    xf = x.astype(jnp.float32)
    mu = jnp.mean(xf, axis=-1, keepdims=True)
    xc = xf - mu
    var = jnp.mean(xc * xc, axis=-1, keepdims=True)
    return (xc * lax.rsqrt(var + EPS)).astype(x.dtype)


def modulate(h, shift, scale):
    return h * (1 + scale) + shift


def dwconv3(x, w):
    xp = jnp.pad(x, ((0, 0), (1, 1), (0, 0)))
    return xp[:, :-2] * w[0] + xp[:, 1:-1] * w[1] + xp[:, 2:] * w[2]


def axial_rope_tables(n):
    rows = n // GRID_W
    row = jnp.repeat(jnp.arange(rows), GRID_W).astype(jnp.float32)
    col = jnp.tile(jnp.arange(GRID_W), rows).astype(jnp.float32)
    inv = ROPE_THETA ** (-2.0 * jnp.arange(ROPE_AXIS_FREQS, dtype=jnp.float32) / ROPE_AXIS_DIM)
    ang = jnp.stack([row[:, None] * inv, col[:, None] * inv], axis=1)
    return jnp.cos(ang), jnp.sin(ang)


def apply_rope(x, cos, sin):
    b, n, h, d = x.shape
    xr = x.astype(jnp.float32).reshape(b, n, h, 2, 2, ROPE_AXIS_FREQS)
    x1, x2 = xr[..., 0, :], xr[..., 1, :]
    cs, sn = cos[None, :, None], sin[None, :, None]
    out = jnp.stack([x1 * cs - x2 * sn, x2 * cs + x1 * sn], axis=-2)
    return out.reshape(b, n, h, d).astype(x.dtype)


def chunk_gmlp(u, v, w_s, b_s):
    b, n, _ = u.shape
    u = jax.nn.gelu(u)
    v = layer_norm_plain(jax.nn.gelu(v).reshape(b, n // CHUNK, CHUNK, A_HEADS, A_HEAD_DIM))
    mixed = jnp.einsum('hpq,bcqhd->bcphd', w_s, v) + b_s.T[None, None, :, :, None]
    return u * mixed.reshape(b, n, A_WIDTH)


def short_gated_conv(proj, conv_w):
    return proj[..., OFF_CB:OFF_CC] * dwconv3(proj[..., OFF_CC:OFF_CH] * proj[..., OFF_CH:D_IN], conv_w)


def q_heads(proj, q_g):
    b, n, _ = proj.shape
    return rms_norm(proj[..., OFF_Q:OFF_K].reshape(b, n, ATT_Q_HEADS, ATT_HEAD_DIM), q_g)


def kv_heads(kv_cols, k_g):
    b, n, _ = kv_cols.shape
    k = rms_norm(kv_cols[..., :ATT_KV_WIDTH].reshape(b, n, ATT_KV_HEADS, ATT_HEAD_DIM), k_g)
    v = kv_cols[..., ATT_KV_WIDTH:].reshape(b, n, ATT_KV_HEADS, ATT_HEAD_DIM)
    return k, v


def latent_attention(q, k_lat, v_lat, k_ctx, v_ctx):
    b, n = q.shape[:2]
    keys = jnp.concatenate([k_ctx, k_lat], axis=1)
    vals = jnp.concatenate([v_ctx, v_lat], axis=1)
    nb = n // Q_BLOCK
    qb = q.reshape(b, nb, Q_BLOCK, ATT_KV_HEADS, ATT_REP, ATT_HEAD_DIM).transpose(1, 0, 2, 3, 4, 5)

    def block(qi):
        s = jnp.einsum('bqgrd,bkgd->bgrqk', qi, keys, preferred_element_type=jnp.float32) * ATT_SCALE
        p = jax.nn.softmax(s, axis=-1).astype(vals.dtype)
        return jnp.einsum('bgrqk,bkgd->bqgrd', p, vals)

    o = lax.map(block, qb)
    return o.transpose(1, 0, 2, 3, 4, 5).reshape(b, n, ATT_WIDTH)


def context_attention(q, k, v):
    b, m = q.shape[:2]
    qg = q.reshape(b, m, ATT_KV_HEADS, ATT_REP, ATT_HEAD_DIM)
    s = jnp.einsum('bqgrd,bkgd->bgrqk', qg, k, preferred_element_type=jnp.float32) * ATT_SCALE
    p = jax.nn.softmax(s, axis=-1).astype(v.dtype)
    return jnp.einsum('bgrqk,bkgd->bqgrd', p, v).reshape(b, m, ATT_WIDTH)


def conv_ffn(h, w_up, w_conv, w_down):
    up = dwconv3(h @ w_up, w_conv)
    a, g = jnp.split(up, 2, axis=-1)
    return (jax.nn.silu(g) * a) @ w_down


def setup_inputs(seed: int = 0) -> dict:
    key = jax.random.key(seed)
    ks = jax.random.split(key, 20)
    f32 = jnp.float32
    nrm = lambda k, shape, s: jax.random.normal(k, shape, f32) * s
    return {
        "x": nrm(ks[0], (BATCH, SEQ, D_MODEL), 1.0),
        "c": nrm(ks[1], (BATCH, D_MODEL), 1.0),
        "ctx": nrm(ks[2], (BATCH, CTX_LEN, D_MODEL), 1.0),
        "c_ctx": nrm(ks[3], (D_MODEL,), 1.0),
        "w_mod": nrm(ks[4], (DEPTH, D_MODEL, N_MOD * D_MODEL), 0.5 * D_MODEL ** -0.5),
        "b_mod": nrm(ks[5], (DEPTH, N_MOD * D_MODEL), 0.02),
        "norm1_g": 1.0 + nrm(ks[6], (DEPTH, D_MODEL), 0.02),
        "w_in": nrm(ks[7], (DEPTH, D_MODEL, D_IN), D_MODEL ** -0.5),
        "q_norm_g": 1.0 + nrm(ks[8], (DEPTH, ATT_HEAD_DIM), 0.02),
        "k_norm_g": 1.0 + nrm(ks[9], (DEPTH, ATT_HEAD_DIM), 0.02),
        "gmlp_w": nrm(ks[10], (DEPTH, A_HEADS, CHUNK, CHUNK), CHUNK ** -0.5),
        "gmlp_b": 1.0 + nrm(ks[11], (DEPTH, A_HEADS, CHUNK), 0.02),
        "conv_c_w": nrm(ks[12], (DEPTH, 3, C_WIDTH), 3 ** -0.5),
        "w_out": nrm(ks[13], (DEPTH, D_MIX, D_MODEL), D_MIX ** -0.5),
        "norm2_g": 1.0 + nrm(ks[14], (DEPTH, D_MODEL), 0.02),
        "ffn_up": nrm(ks[15], (DEPTH, D_MODEL, 2 * D_FF), D_MODEL ** -0.5),
        "ffn_conv_w": nrm(ks[16], (DEPTH, 3, 2 * D_FF), 3 ** -0.5),
        "ffn_down": nrm(ks[17], (DEPTH, D_FF, D_MODEL), D_FF ** -0.5),
        "final_g": 1.0 + nrm(ks[18], (D_MODEL,), 0.02),
    }


def reference(x, c, ctx, c_ctx, w_mod, b_mod, norm1_g, w_in, q_norm_g, k_norm_g, gmlp_w, gmlp_b,
              conv_c_w, w_out, norm2_g, ffn_up, ffn_conv_w, ffn_down, final_g):
    b, n, d = x.shape
    cos, sin = axial_rope_tables(n)
    silu_c = jax.nn.silu(c)
    silu_cc = jax.nn.silu(c_ctx)
    xc = ctx
    for l in range(DEPTH):
        last = l == DEPTH - 1
        mod = (silu_c @ w_mod[l] + b_mod[l]).reshape(b, N_MOD, 1, d)
        mod_c = (silu_cc @ w_mod[l] + b_mod[l]).reshape(N_MOD, d)
        sh1, sc1, g1, sh2, sc2, g2 = [mod[:, i] for i in range(N_MOD)]
        csh1, csc1, cg1, csh2, csc2, cg2 = [mod_c[i] for i in range(N_MOD)]

        hc = modulate(rms_norm(xc, norm1_g[l]), csh1, csc1)
        if last:
            k_c, v_c = kv_heads(hc @ w_in[l][:, OFF_K:OFF_CB], k_norm_g[l])
        else:
            proj_c = hc @ w_in[l]
            k_c, v_c = kv_heads(proj_c[..., OFF_K:OFF_CB], k_norm_g[l])
            a_c = chunk_gmlp(proj_c[..., OFF_AU:OFF_AV], proj_c[..., OFF_AV:OFF_Q], gmlp_w[l], gmlp_b[l])
            att_c = context_attention(q_heads(proj_c, q_norm_g[l]), k_c, v_c)
            cm_c = short_gated_conv(proj_c, conv_c_w[l])
            mix_c = jnp.concatenate([a_c, att_c, cm_c], axis=-1) @ w_out[l]

        h = modulate(rms_norm(x, norm1_g[l]), sh1, sc1)
        proj = h @ w_in[l]
        q = apply_rope(q_heads(proj, q_norm_g[l]), cos, sin)
        k, v = kv_heads(proj[..., OFF_K:OFF_CB], k_norm_g[l])
        k = apply_rope(k, cos, sin)
        att = latent_attention(q, k, v, k_c, v_c)
        a = chunk_gmlp(proj[..., OFF_AU:OFF_AV], proj[..., OFF_AV:OFF_Q], gmlp_w[l], gmlp_b[l])
        cm = short_gated_conv(proj, conv_c_w[l])
        x = x + g1 * (jnp.concatenate([a, att, cm], axis=-1) @ w_out[l])
        x = x + g2 * conv_ffn(modulate(rms_norm(x, norm2_g[l]), sh2, sc2), ffn_up[l], ffn_conv_w[l], ffn_down[l])

        if not last:
            xc = xc + cg1 * mix_c
            xc = xc + cg2 * conv_ffn(modulate(rms_norm(xc, norm2_g[l]), csh2, csc2), ffn_up[l], ffn_conv_w[l], ffn_down[l])
    return rms_norm(x, final_g)
```

```python
import math
import numpy as np
from contextlib import ExitStack
import concourse.bass as bass
import concourse.mybir as mybir
from concourse.bass_utils import run_bass_kernel_spmd

F32 = mybir.dt.float32
BF16 = mybir.dt.bfloat16
I32 = mybir.dt.int32
AF = mybir.ActivationFunctionType
ALU = mybir.AluOpType

D = 1024
KC = 8
NCTX = 256
NLAT = 2048
NT = NCTX + NLAT
DEPTH = 4
DFF = 2816
NJ = 22
EPS = 1e-6
VL = 252
NV = 1024
O_N1, O_N2, O_BM, O_FCW, O_CCW, O_QG, O_KG = 0, 8, 16, 112, 244, 250, 251
O_FG = DEPTH * VL
O_FIDX = O_FG + 8
XW = 4100
GELU_C = 2.0 * math.sqrt(2.0 / math.pi)


class Op:
    __slots__ = ("eng", "fn", "r", "w", "dma", "deps", "signal", "sem", "val", "idx", "lane")

    def __init__(self, eng, fn, r, w, dma):
        self.eng = eng
        self.fn = fn
        self.r = tuple(r)
        self.w = tuple(w)
        self.dma = dma
        self.deps = []
        self.signal = bool(dma)
        self.sem = None
        self.val = 0
        self.lane = None


class Sched:
    ENGS = ("pe", "act", "dve", "pool", "sp")
    NLANES = 8

    def __init__(self, nc):
        self.nc = nc
        self.ops = []
        self.cc_sems = []

    def op(self, eng, fn, r=(), w=(), dma=False):
        o = Op(eng, fn, r, w, dma)
        o.idx = len(self.ops)
        self.ops.append(o)
        return o

    def pe(self, fn, r=(), w=()):
        return self.op("pe", fn, r, w)

    def act(self, fn, r=(), w=()):
        return self.op("act", fn, r, w)

    def dve(self, fn, r=(), w=()):
        return self.op("dve", fn, r, w)

    def pool(self, fn, r=(), w=()):
        return self.op("pool", fn, r, w)

    def dma(self, fn, r=(), w=(), q="sp"):
        return self.op(q, fn, r, w, dma=True)

    def _analyze(self):
        last_w = {}
        readers = {}
        dma_count = {e: 0 for e in self.ENGS}
        dma_hist = {e: [] for e in self.ENGS}
        for o in self.ops:
            deps = set()
            for t in o.r:
                if t in last_w:
                    deps.add(last_w[t])
            for t in o.w:
                if t in last_w:
                    deps.add(last_w[t])
                rd = readers.get(t)
                if rd:
                    deps.update(rd[0].values())
                    deps.update(rd[1])
            if o.dma == "cc":
                o.val = 1
            elif o.dma:
                i = dma_count[o.eng]
                o.lane = i % self.NLANES
                o.val = 16 * (i // self.NLANES + 1)
                if i >= self.NLANES:
                    deps.add(dma_hist[o.eng][i - self.NLANES])
                dma_hist[o.eng].append(o.idx)
                dma_count[o.eng] += 1
            deps.discard(o.idx)
            fin = []
            for d in deps:
                Dd = self.ops[d]
                if Dd.eng == o.eng and not Dd.dma and not o.dma:
                    if o.eng == "pe":
                        continue
                    raw = any(t in Dd.w for t in o.r) or any(t in Dd.w for t in o.w)
                    if not raw:
                        continue
                fin.append(d)
                Dd.signal = True
            o.deps = fin
            for t in o.w:
                last_w[t] = o.idx
                readers[t] = ({}, [])
            for t in o.r:
                if t not in o.w:
                    rd = readers.setdefault(t, ({}, []))
                    if o.dma:
                        rd[1].append(o.idx)
                    else:
                        rd[0][o.eng] = o.idx
        cnt = {e: 0 for e in self.ENGS}
        for o in self.ops:
            if o.dma:
                continue
            if o.signal:
                cnt[o.eng] += 1
                o.val = cnt[o.eng]

    def emit(self, sems_ctx):
        self._analyze()
        nc = self.nc
        ncc = 0
        for o in self.ops:
            if o.dma == "cc":
                o.sem = self.cc_sems[ncc]
                ncc += 1
            elif o.dma:
                o.sem = sems_ctx["lane_%s_%d" % (o.eng, o.lane)]
            else:
                o.sem = sems_ctx["clk_" + o.eng]
        by_eng = {e: [o for o in self.ops if o.eng == e] for e in self.ENGS}
        ops = self.ops

        def run(engname, eng):
            known = {}
            for o in by_eng[engname]:
                need = {}
                for d in o.deps:
                    Dd = ops[d]
                    k = id(Dd.sem)
                    if known.get(k, 0) >= Dd.val:
                        continue
                    if k not in need or need[k][1] < Dd.val:
                        need[k] = (Dd.sem, Dd.val)
                for k, (s, v) in need.items():
                    eng.wait_ge(s, v)
                    known[k] = v
                ins = o.fn(eng)
                if ins is None:
                    continue
                if o.dma == "cc":
                    ins.then_inc(o.sem)
                elif o.dma:
                    ins.then_inc(o.sem, 16)
                elif o.signal:
                    ins.then_inc(o.sem, 1)

        with nc.Block() as block:
            @block.sync
            def _(e):
                run("sp", e)

            @block.tensor
            def _(e):
                run("pe", e)

            @block.scalar
            def _(e):
                run("act", e)

            @block.vector
            def _(e):
                run("dve", e)

            @block.gpsimd
            def _(e):
                run("pool", e)


def build_nc(depth=DEPTH, dbg=None, upto=None):
    nc = bass.Bass("TRN2", target_bir_lowering=False)

    def din(name, shape, dt=F32):
        return nc.dram_tensor(name, list(shape), dt, kind="ExternalInput").ap()

    xT_in = din("xT_in", [128, KC, NLAT])
    cT_in = din("cT_in", [128, KC, NCTX])
    cvT_in = din("cvT_in", [128, KC, 2])
    rowcol = din("rowcol", [2, NLAT])
    sel_in = din("sel", [128, 8])
    vec_in = din("vecT", [128, NV])
    consts_in = din("consts", [128, 3, 128])
    swap_in = din("swapm", [128, 128])
    wsT_in = din("wsT", [depth, 128, 4, 128])
    gb_in = din("gbias", [depth, 128, 2, 128])
    w_mod = din("w_mod", [depth, D, 6 * D])
    w_in = din("w_in", [depth, D, 2048])
    w_out = din("w_out", [depth, D, D])
    ffn_up = din("ffn_up", [depth, D, 2 * DFF])
    ffn_down = din("ffn_down", [depth, DFF, D])
    yT = nc.dram_tensor("yT", [128, KC, NLAT], F32, kind="ExternalOutput").ap()
    dbg_out = {}
    if dbg:
        for nm, shp in dbg.items():
            dbg_out[nm] = nc.dram_tensor("dbg_" + nm, list(shp), F32, kind="ExternalOutput").ap()

    agA_in = nc.dram_tensor("agA_in", [128, 2052], BF16)
    agA_out = nc.dram_tensor("agA_out", [512, 2052], BF16)
    agB_in = nc.dram_tensor("agB_in", [128, 2048], BF16)
    agB_out = nc.dram_tensor("agB_out", [512, 2048], BF16)
    ag2_in = nc.dram_tensor("ag2_in", [128, 16], BF16)
    ag2_out = nc.dram_tensor("ag2_out", [512, 16], BF16)
    kctx = nc.dram_tensor("kctx", [128, NCTX], BF16)
    vctx = nc.dram_tensor("vctx", [128, 2, 128], BF16)
    tabscr = nc.dram_tensor("tabscr", [128, 4096], F32)

    es = ExitStack()
    with es:
        def sb(name, shape, dt):
            return es.enter_context(nc.sbuf_tensor(name, list(shape), dt))

        S = Sched(nc)
        sems = {}
        for e in Sched.ENGS:
            sems["clk_" + e] = es.enter_context(nc.semaphore("clk_" + e))
            for i in range(Sched.NLANES):
                sems["lane_%s_%d" % (e, i)] = es.enter_context(nc.semaphore("lane_%s_%d" % (e, i)))
        for i in range(3 * depth + 1):
            S.cc_sems.append(es.enter_context(nc.semaphore("cc%d" % i)))

        xT = sb("xT", [128, KC, NT], F32)
        KT = sb("KT", [128, 8448], BF16)
        VA = sb("VA", [128, 133 * 64], BF16)
        U = sb("U", [128, 23048], BF16)
        hT = sb("hT", [128, KC, 512], BF16)
        WSL = 2816
        wbuf = [sb("wbuf%d" % i, [128, WSL], BF16) for i in range(2)]
        ptl = [sb("pt%d" % i, [128, 1024], BF16) for i in range(2)]
        NTMP = 5
        tmp = [sb("tmp%d" % i, [128, 513], F32) for i in range(NTMP)]
        bt = [sb("bt%d" % i, [128, 512], BF16) for i in range(4)]
        gv = sb("gv", [128, 256], F32)
        gv2 = sb("gv2", [128, 256], F32)
        vn = sb("vn", [128, 256], BF16)
        vst = sb("vst", [128, 128], BF16)
        wsT = sb("wsT_sb", [128, 4, 128], BF16)
        gbias = sb("gbias_sb", [128, 2, 128], F32)
        consts = sb("consts_sb", [128, 3, 128], BF16)
        vecT = sb("vecT_sb", [128, NV], F32)
        modT = sb("modT", [128, DEPTH, 48, 2], F32)
        A1 = sb("A1", [128, DEPTH, KC, 2], F32)
        A2 = sb("A2", [128, DEPTH, KC, 2], F32)
        cvs = sb("cvs", [128, KC, 2], F32)
        cvb = sb("cvb", [128, KC, 2], BF16)
        sel = sb("sel_sb", [128, 8], F32)
        small = sb("small", [128, 64], F32)
        saves = sb("saves", [128, 2, 44, 2], F32)
        uph = sb("uph", [128, 44, 2], F32)
        hal1 = sb("hal1", [128, 4, 4], BF16)
        hs = sb("hs", [128, 4], BF16)
        swapF = sb("swapF", [128, 128], F32)
        ag1_toks = []
        kvc_toks = []
        hal2 = sb("hal2", [128, 4, 16], BF16)
        h2h = sb("h2h", [128, KC, 2], BF16)
        xh = sb("xh", [128, KC, 2], F32)
        bnst = sb("bnst", [128, 4, 6], F32)
        bnag = sb("bnag", [128, 4, 2], F32)
        ps = es.enter_context(nc.psum_tensor("ps", [128, 4096], F32))

        def PS(i, n=512, p0=0, p1=128):
            return ps[p0:p1, i * 512:i * 512 + n]

        onesB = consts[:, 0, :]
        blkones = consts[:, 1, :]
        Rrot = consts[:, 2, :]
        QT = U[:, 0:9216].rearrange("p (c t) -> p c t", c=4)
        aT = U[:, 9216:13824].rearrange("p (c t) -> p c t", c=2)
        PT = U[:, 13824:18440].rearrange("p (c t) -> p c t", c=2)
        CBT = U[:, 18440:23048].rearrange("p (c t) -> p c t", c=2)
        actT = U[:, 0:22 * 513].rearrange("p (j t) -> p j t", j=22)
        UTOK = ["QT", "aT", "PT", "CBT", "actT"]
        tabF = KT[:].bitcast(F32)
        cosT = tabF[:, 0:2048]
        sinT = tabF[:, 2048:4096]
        epsc = small[:, 0:1]

        def vcol(l, off, n=1):
            return vecT[:, l * VL + off:l * VL + off + n]

        blocks = [(0, NCTX, True)] + [(NCTX + 512 * i, 512, False) for i in range(4)]

        steps = []

        stage = ["pro"]

        def add_step(loads, fn):
            steps.append((loads, fn, stage[0]))

        def ag1_tok():
            t = "ag1_%d" % len(ag1_toks)
            ag1_toks.append(t)
            return t

        def agb_tok():
            t = "agb_%d" % len(kvc_toks)
            kvc_toks.append(t)
            return t

        KTOK = ["KT%d" % i for i in range(10)]
        VTOK = ["VA%d" % i for i in range(5)]

        def dbg_dump(name, src_ap, toks):
            if name in dbg_out:
                S.dma(lambda e: e.dma_start(out=dbg_out[name], in_=src_ap), r=toks, w=["dbg_" + name])

        def rms_stats(src_fn, n, out_rstd, rtok, tag, psb=0, ones=None, scale=1.0 / D):
            ones_ap = onesB if ones is None else ones
            for k in range(KC):
                b = bt[k % 2]
                S.act(lambda e, k=k, b=b: e.activation(out=b[:, 0:n], in_=src_fn(k), func=AF.Square),
                      r=rtok, w=["bt%d" % (k % 2)])
                S.pe(lambda e, k=k, b=b: e.matmul(PS(psb, n), lhsT=ones_ap, rhs=b[:, 0:n],
                                                  start=(k == 0), stop=(k == KC - 1)),
                     r=["bt%d" % (k % 2), "consts"], w=["ps%d" % psb])
            S.act(lambda e: e.activation(out=out_rstd, in_=PS(psb, n), func=AF.Sqrt, scale=scale, bias=epsc),
                  r=["ps%d" % psb, "small"], w=[tag])
            S.dve(lambda e: e.reciprocal(out=out_rstd, in_=out_rstd), r=[tag], w=[tag])

        def norm_mod(l, c0, n, j, Asel, shoff, dst_fn, dtok, rstd_tmp=0, t_tmp=1):
            rstd = tmp[rstd_tmp][:, 0:n]
            rms_stats(lambda k: xT[:, k, c0:c0 + n], n, rstd, ["x"], "tmp%d" % rstd_tmp)
            for k in range(KC):
                tt = tmp[t_tmp + (k % 2)]
                S.dve(lambda e, k=k, tt=tt: e.scalar_tensor_tensor(
                    out=tt[:, 0:n], in0=xT[:, k, c0:c0 + n], scalar=Asel[:, l, k, j:j + 1], in1=rstd,
                    op0=ALU.mult, op1=ALU.mult), r=["x", "tmp%d" % rstd_tmp, "A"], w=["tmp%d" % (t_tmp + k % 2)])
                S.act(lambda e, k=k, tt=tt: e.activation(
                    out=dst_fn(k), in_=tt[:, 0:n], func=AF.Identity,
                    bias=modT[:, l, shoff * 8 + k, j:j + 1], scale=1.0),
                    r=["tmp%d" % (t_tmp + k % 2), "modT"], w=[dtok])

        def gelu_from_psum(psb, n, out_ap, otok, p=128, ta=3, tb=4):
            src = ps[0:p, psb * 512:psb * 512 + n]
            a = tmp[ta][0:p, 0:n]
            b = tmp[tb][0:p, 0:n]
            S.act(lambda e: e.activation(out=a, in_=src, func=AF.Square), r=["ps%d" % psb], w=["tmp%d" % ta])
            S.dve(lambda e: e.tensor_scalar(out=a, in0=a, scalar1=0.044715, scalar2=1.0, op0=ALU.mult, op1=ALU.add),
                  r=["tmp%d" % ta], w=["tmp%d" % ta])
            S.dve(lambda e: e.tensor_tensor(out=a, in0=src, in1=a, op=ALU.mult),
                  r=["tmp%d" % ta, "ps%d" % psb], w=["tmp%d" % ta])
            S.act(lambda e: e.activation(out=b, in_=a, func=AF.Sigmoid, scale=GELU_C), r=["tmp%d" % ta], w=["tmp%d" % tb])
            S.dve(lambda e: e.tensor_tensor(out=out_ap, in0=src, in1=b, op=ALU.mult),
                  r=["tmp%d" % tb, "ps%d" % psb], w=[otok])

        def proj_fm(piece_fn, nk, rhs_fn, n, psb, rtoks, m0=0, m1=128, p0=0, p1=128):
            for k in range(nk):
                S.pe(lambda e, k=k: e.matmul(ps[p0:p1, psb * 512:psb * 512 + n], lhsT=piece_fn(k)[:, m0:m1],
                                             rhs=rhs_fn(k), start=(k == 0), stop=(k == nk - 1)),
                     r=rtoks, w=["ps%d" % psb])

        def headnorm_rope(l, psb, n, gcol, latent, c0, out_ap, otok):
            src = PS(psb, n)
            sq = bt[2]
            S.act(lambda e: e.activation(out=sq[:, 0:n], in_=src, func=AF.Square), r=["ps%d" % psb], w=["bt2"])
            S.pe(lambda e: e.matmul(PS(5, n), lhsT=blkones, rhs=sq[:, 0:n], start=True, stop=True),
                 r=["bt2", "consts"], w=["ps5"])
            rs = tmp[0][:, 0:n]
            S.act(lambda e: e.activation(out=rs, in_=PS(5, n), func=AF.Sqrt, scale=1.0 / 64, bias=epsc),
                  r=["ps5", "small"], w=["tmp0"])
            S.dve(lambda e: e.reciprocal(out=rs, in_=rs), r=["tmp0"], w=["tmp0"])
            qn = tmp[1][:, 0:n]
            S.dve(lambda e: e.scalar_tensor_tensor(out=qn, in0=src, scalar=gcol, in1=rs, op0=ALU.mult, op1=ALU.mult),
                  r=["ps%d" % psb, "tmp0", "vecT"], w=["tmp1"])
            if not latent:
                S.act(lambda e: e.copy(out=out_ap, in_=qn), r=["tmp1"], w=[otok])
                return
            qb = bt[3]
            S.act(lambda e: e.copy(out=qb[:, 0:n], in_=qn), r=["tmp1"], w=["bt3"])
            S.pe(lambda e: e.matmul(PS(4, n), lhsT=Rrot, rhs=qb[:, 0:n], start=True, stop=True),
                 r=["bt3", "consts"], w=["ps4"])
            tc0 = c0 - NCTX
            t1 = tmp[2][:, 0:n]
            t2 = tmp[3][:, 0:n]
            S.dve(lambda e: e.tensor_tensor(out=t1, in0=qn, in1=cosT[:, tc0:tc0 + n], op=ALU.mult),
                  r=["tmp1", "tab"], w=["tmp2"])
            S.dve(lambda e: e.tensor_tensor(out=t2, in0=PS(4, n), in1=sinT[:, tc0:tc0 + n], op=ALU.mult),
                  r=["ps4", "tab"], w=["tmp3"])
            S.pool(lambda e: e.tensor_tensor(out=out_ap, in0=t1, in1=t2, op=ALU.add), r=["tmp2", "tmp3"], w=[otok])

        S.dma(lambda e: e.dma_start(out=xT[:, :, NCTX:NT], in_=xT_in), w=["x"], q="sp")
        S.dma(lambda e: e.dma_start(out=xT[:, :, 0:NCTX], in_=cT_in), w=["x"], q="act")
        S.dma(lambda e: e.dma_start(out=vecT[:], in_=vec_in), w=["vecT"], q="sp")
        S.dma(lambda e: e.dma_start(out=sel[:], in_=sel_in), w=["sel"], q="sp")
        S.dma(lambda e: e.dma_start(out=cvs[:], in_=cvT_in), w=["cvs"], q="sp")
        S.dma(lambda e: e.dma_start(out=consts[:], in_=consts_in), w=["consts"], q="pool")
        S.dma(lambda e: e.dma_start(out=swapF[:], in_=swap_in), w=["swapF"], q="sp")
        S.dve(lambda e: e.memset(small[:], 0.0), w=["small"])
        S.dve(lambda e: e.memset(small[:, 0:1], EPS), r=["small"], w=["small"])
        S.pool(lambda e: e.memset(VA[:], 1.0), w=VTOK)
        S.pool(lambda e: e.memset(U[:], 0.0), w=UTOK)
        S.pool(lambda e: e.memset(saves[:], 0.0), w=["saves"])
        S.pool(lambda e: e.memset(uph[:], 0.0), w=["uph"])

        posb = tmp[0]
        S.act(lambda e: e.activation(out=small[:, 1:2], in_=vecT[:, O_FIDX:O_FIDX + 1], func=AF.Exp,
                                     scale=-math.log(10000.0) / 16.0), r=["vecT", "small"], w=["small"])
        TWO_PI = 2.0 * math.pi
        for qd in range(4):
            q0 = qd * 512
            pos = tmp[0][:, 0:512]
            for pb, ax in ((0, 0), (32, 1), (64, 0), (96, 1)):
                S.dma(lambda e, pb=pb, ax=ax, q0=q0: e.dma_start(
                    out=tmp[0][pb:pb + 32, 0:512], in_=rowcol[ax, q0:q0 + 512].partition_broadcast(32)),
                    w=["tmp0"], q="sp")
            ang = tmp[1][:, 0:512]
            S.dve(lambda e: e.tensor_scalar(out=ang, in0=pos, scalar1=small[:, 1:2], scalar2=None, op0=ALU.mult),
                  r=["tmp0", "small"], w=["tmp1"])
            for which, dst in ((0, sinT), (1, cosT)):
                a2 = tmp[2][:, 0:512]
                kf = tmp[3][:, 0:512]
                ki = tmp[4][:, 0:512].bitcast(I32)
                S.dve(lambda e, which=which: e.tensor_scalar(out=a2, in0=ang, scalar1=1.0, scalar2=which * math.pi / 2,
                                                             op0=ALU.mult, op1=ALU.add), r=["tmp1"], w=["tmp2"])
                S.dve(lambda e: e.tensor_scalar(out=kf, in0=a2, scalar1=1.0 / TWO_PI, scalar2=None, op0=ALU.mult),
                      r=["tmp2"], w=["tmp3"])
                S.dve(lambda e: e.tensor_copy(out=ki, in_=kf), r=["tmp3"], w=["tmp4"])
                S.dve(lambda e: e.tensor_copy(out=kf, in_=ki), r=["tmp4"], w=["tmp3"])
                S.dve(lambda e: e.scalar_tensor_tensor(out=a2, in0=kf, scalar=-TWO_PI, in1=a2, op0=ALU.mult, op1=ALU.add),
                      r=["tmp3", "tmp2"], w=["tmp2"])
                S.dve(lambda e: e.tensor_scalar(out=kf, in0=a2, scalar1=math.pi, scalar2=-TWO_PI, op0=ALU.is_gt, op1=ALU.mult),
                      r=["tmp2"], w=["tmp3"])
                S.dve(lambda e: e.tensor_tensor(out=a2, in0=a2, in1=kf, op=ALU.add), r=["tmp2", "tmp3"], w=["tmp2"])
                S.dve(lambda e: e.tensor_scalar(out=kf, in0=a2, scalar1=-math.pi, scalar2=TWO_PI, op0=ALU.is_lt, op1=ALU.mult),
                      r=["tmp2"], w=["tmp3"])
                S.dve(lambda e: e.tensor_tensor(out=a2, in0=a2, in1=kf, op=ALU.add), r=["tmp2", "tmp3"], w=["tmp2"])
                S.act(lambda e, dst=dst, q0=q0: e.activation(out=dst[:, q0:q0 + 512], in_=a2, func=AF.Sin),
                      r=["tmp2"], w=["tab"])
        S.dma(lambda e: e.dma_start(out=tabscr.ap(), in_=tabF[:, 0:4096]), r=["tab"], w=["tabscr"], q="sp")

        S.act(lambda e: e.activation(out=cvb[:], in_=cvs[:], func=AF.Silu), r=["cvs"], w=["cvb"])
        for l in range(depth):
            for pc in range(24):
                def mod_fn(slot, stok, l=l, pc=pc):
                    sv = slot[:, 0:KC * 256].rearrange("p (k n) -> p k n", k=KC)
                    for cc in range(2):
                        c = pc * 2 + cc
                        for k in range(KC):
                            S.pe(lambda e, k=k, cc=cc, c=c: e.matmul(
                                ps[:, 3584 + c * 2:3584 + c * 2 + 2], lhsT=sv[:, k, cc * 128:(cc + 1) * 128],
                                rhs=cvb[:, k, :], start=(k == 0), stop=(k == KC - 1)),
                                r=[stok, "cvb"], w=["ps7"])
                    if pc == 23:
                        S.dve(lambda e: e.tensor_tensor(
                            out=modT[:, l].rearrange("p c j -> p (c j)"), in0=ps[:, 3584:3584 + 96],
                            in1=vecT[:, l * VL + O_BM:l * VL + O_BM + 96], op=ALU.add),
                            r=["ps7", "vecT"], w=["modT"])
                        for (Ax, noff, scoff) in ((A1, O_N1, 1), (A2, O_N2, 4)):
                            for j in range(2):
                                S.dve(lambda e, Ax=Ax, noff=noff, scoff=scoff, j=j: e.scalar_tensor_tensor(
                                    out=Ax[:, l, :, j], in0=modT[:, l, scoff * 8:scoff * 8 + 8, j], scalar=1.0,
                                    in1=vecT[:, l * VL + noff:l * VL + noff + 8], op0=ALU.add, op1=ALU.mult),
                                    r=["modT", "vecT"], w=["A"])
                src = w_mod[l].rearrange("(k p) n -> p k n", p=128)[:, :, pc * 256:(pc + 1) * 256]
                add_step([(src, lambda slot: slot[:, 0:KC * 256].rearrange("p (k n) -> p k n", k=KC))], mod_fn)

        def layer(l):
            last = (l == depth - 1)
            winv = w_in[l].rearrange("(k p) n -> p k n", p=128)
            woutv = w_out[l].rearrange("(k p) n -> p k n", p=128)
            upv = ffn_up[l].rearrange("(k p) n -> p k n", p=128)
            dnv = ffn_down[l].rearrange("(j p) n -> p j n", p=128)

            def pre_layer(slot, stok):
                if l > 0:
                    S.dma(lambda e: e.dma_start(out=tabF[:, 0:4096], in_=tabscr.ap()),
                          r=["tabscr"], w=["tab"] + KTOK, q="sp")
                    S.pool(lambda e: e.memset(small[:, 8:9], 0.0), r=UTOK, w=UTOK + ["small"])
                S.dma(lambda e: e.dma_start(out=wsT[:], in_=wsT_in[l]), w=["wsT"], q="pool")
                S.dma(lambda e: e.dma_start(out=gbias[:], in_=gb_in[l]), w=["gbias"], q="sp")
            stage[0] = "m1"
            add_step([], pre_layer)

            for bi, (c0, n, isctx) in enumerate(blocks):
                j = 1 if isctx else 0
                nt = n // 128

                def m1_norm(slot, stok, c0=c0, n=n, j=j):
                    norm_mod(l, c0, n, j, A1, 0, lambda k: hT[:, k, 0:n], "h")
                add_step([], m1_norm)
                hk = lambda k, n=n: hT[:, k, 0:n]

                def pv(slot):
                    return slot[:, 0:KC * 256].rearrange("p (k n) -> p k n", k=KC)

                def piece_src(pi):
                    return winv[:, :, pi * 256:(pi + 1) * 256]

                def f_au(slot, stok, c0=c0, n=n, hk=hk):
                    sv = pv(slot)
                    for m in range(2):
                        proj_fm(lambda k: sv[:, k, :], KC, hk, n, 1 + m, [stok, "h"], m * 128, (m + 1) * 128)
                        gelu_from_psum(1 + m, n, aT[:, m, c0:c0 + n], "aT")
                add_step([(piece_src(0), pv)], f_au)

                def f_av(slot, stok, c0=c0, n=n, nt=nt):
                    sv = pv(slot)
                    for tt in range(nt):
                        for k in range(KC):
                            S.pe(lambda e, k=k, tt=tt: e.matmul(PS(6, 256), lhsT=hT[:, k, tt * 128:(tt + 1) * 128],
                                                                rhs=sv[:, k, :], start=(k == 0), stop=(k == KC - 1)),
                                 r=[stok, "h"], w=["ps6"])
                        gelu_from_psum(6, 256, gv[:], "gv")
                        for h in range(4):
                            S.dve(lambda e, h=h: e.bn_stats(out=bnst[:, h, :], in_=gv[:, h * 64:(h + 1) * 64]),
                                  r=["gv"], w=["bnst"])
                        for h in range(4):
                            S.dve(lambda e, h=h: e.bn_aggr(out=bnag[:, h, :], in_=bnst[:, h, :]), r=["bnst"], w=["bnag"])
                        S.act(lambda e: e.activation(out=small[:, 16:20], in_=bnag[:, :, 1], func=AF.Sqrt,
                                                     bias=epsc, scale=1.0), r=["bnag", "small"], w=["small"])
                        S.dve(lambda e: e.reciprocal(out=small[:, 16:20], in_=small[:, 16:20]), r=["small"], w=["small"])
                        for h in range(4):
                            S.dve(lambda e, h=h: e.tensor_scalar(
                                out=vn[:, h * 64:(h + 1) * 64], in0=gv[:, h * 64:(h + 1) * 64],
                                scalar1=bnag[:, h, 0:1], scalar2=small[:, 16 + h:17 + h],
                                op0=ALU.subtract, op1=ALU.mult), r=["gv", "bnag", "small"], w=["vn"])
                        for pr in range(2):
                            for hh in range(2):
                                h = 2 * pr + hh
                                S.pe(lambda e, h=h, hh=hh: e.matmul(
                                    ps[hh * 64:(hh + 1) * 64, 7 * 512:7 * 512 + 128], lhsT=vn[:, h * 64:(h + 1) * 64],
                                    rhs=wsT[:, h, :], start=True, stop=True), r=["vn", "wsT"], w=["ps7"])
                            cs = c0 + tt * 128
                            S.dve(lambda e, pr=pr: e.tensor_tensor(out=gv2[:, 0:128], in0=PS(7, 128), in1=gbias[:, pr, :],
                                                                   op=ALU.add), r=["ps7", "gbias"], w=["gv2"])
                            S.pool(lambda e, pr=pr, cs=cs: e.tensor_tensor(
                                out=aT[:, pr, cs:cs + 128], in0=gv2[:, 0:128], in1=aT[:, pr, cs:cs + 128], op=ALU.mult),
                                r=["gv2", "aT"], w=["aT"])
                add_step([(piece_src(1), pv)], f_av)

                for qp in range(2):
                    def f_q(slot, stok, qp=qp, c0=c0, n=n, hk=hk, isctx=isctx):
                        sv = pv(slot)
                        for mm in range(2):
                            m = qp * 2 + mm
                            psb = 1 + mm
                            proj_fm(lambda k: sv[:, k, :], KC, hk, n, psb, [stok, "h"], mm * 128, (mm + 1) * 128)
                            headnorm_rope(l, psb, n, vcol(l, O_QG), not isctx, c0, QT[:, m, c0:c0 + n], "QT")
                    add_step([(piece_src(2 + qp), pv)], f_q)

                def f_kv(slot, stok, c0=c0, n=n, hk=hk, isctx=isctx, nt=nt):
                    sv = pv(slot)
                    for g in range(2):
                        psb = 1 + g
                        for half in range(2):
                            proj_fm(lambda k: sv[:, k, :], KC, hk, n, psb, [stok, "h"], g * 64, (g + 1) * 64,
                                    half * 64, (half + 1) * 64)
                        kb = bt[g]
                        headnorm_rope(l, psb, n, vcol(l, O_KG), not isctx, c0, kb[:, 0:n], "bt%d" % g)
                        if isctx:
                            S.dma(lambda e, g=g, kb=kb: e.dma_start(out=kctx.ap()[g * 64:(g + 1) * 64, :], in_=kb[0:64, 0:n]),
                                  r=["bt%d" % g], w=["kctx%d" % g], q="sp")
                        else:
                            tc0 = c0 - NCTX
                            S.dma(lambda e, g=g, kb=kb, tc0=tc0: e.dma_start(
                                out=agA_in.ap()[g * 64:(g + 1) * 64, tc0:tc0 + n], in_=kb[0:64, 0:n]),
                                r=["bt%d" % g], w=[ag1_tok()], q="sp")
                    for tt in range(nt):
                        for k in range(KC):
                            S.pe(lambda e, k=k, tt=tt: e.matmul(PS(6, 128), lhsT=hT[:, k, tt * 128:(tt + 1) * 128],
                                                                rhs=sv[:, k, 128:256], start=(k == 0), stop=(k == KC - 1)),
                                 r=[stok, "h"], w=["ps6"])
                        S.act(lambda e: e.copy(out=vst[:], in_=PS(6, 128)), r=["ps6"], w=["vst"])
                        if isctx:
                            S.dma(lambda e, tt=tt: e.dma_start(out=vctx.ap()[:, tt, :], in_=vst[:]), r=["vst"], w=["vctx%d" % tt], q="act")
                        else:
                            tg = (c0 - NCTX) // 128 + tt
                            S.dma(lambda e, tg=tg: e.dma_start(out=agB_in.ap()[:, tg * 128:(tg + 1) * 128],
                                                               in_=vst[:]), r=["vst"], w=[agb_tok()], q="act")
                add_step([(piece_src(4), pv)], f_kv)

                pc0 = (1 + c0) if isctx else (259 + c0 - NCTX)

                def f_cb(slot, stok, c0=c0, n=n, hk=hk):
                    sv = pv(slot)
                    for m in range(2):
                        proj_fm(lambda k: sv[:, k, :], KC, hk, n, 1 + m, [stok, "h"], m * 128, (m + 1) * 128)
                        S.act(lambda e, m=m: e.copy(out=CBT[:, m, c0:c0 + n], in_=PS(1 + m, n)), r=["ps%d" % (1 + m)], w=["CBT"])
                add_step([(piece_src(5), pv)], f_cb)

                def f_cc(slot, stok, n=n, hk=hk):
                    sv = pv(slot)
                    for m in range(2):
                        proj_fm(lambda k: sv[:, k, :], KC, hk, n, 1 + m, [stok, "h"], m * 128, (m + 1) * 128)
                        S.act(lambda e, m=m: e.copy(out=tmp[3 + m][:, 0:n], in_=PS(1 + m, n)),
                              r=["ps%d" % (1 + m)], w=["tmp%d" % (3 + m)])
                add_step([(piece_src(6), pv)], f_cc)

                def f_ch(slot, stok, n=n, hk=hk, pc0=pc0):
                    sv = pv(slot)
                    for m in range(2):
                        proj_fm(lambda k: sv[:, k, :], KC, hk, n, 1 + m, [stok, "h"], m * 128, (m + 1) * 128)
                        S.dve(lambda e, m=m: e.tensor_tensor(out=PT[:, m, pc0:pc0 + n], in0=PS(1 + m, n),
                                                             in1=tmp[3 + m][:, 0:n], op=ALU.mult),
                              r=["ps%d" % (1 + m), "tmp%d" % (3 + m)], w=["PT"])
                add_step([(piece_src(7), pv)], f_ch)

            def exch1(slot, stok):
                hsv = hs[:].rearrange("p (c t) -> p c t", c=2)
                S.dve(lambda e: e.tensor_copy(out=hsv[:, :, 0], in_=PT[:, :, 259]), r=["PT"], w=["hs"])
                S.dve(lambda e: e.tensor_copy(out=hsv[:, :, 1], in_=PT[:, :, 2306]), r=["PT"], w=["hs"])
                S.dma(lambda e: e.dma_start(out=agA_in.ap()[:, 2048:2052], in_=hs[:]), r=["hs"], w=["ag1_halo"], q="sp")
                S.op("pool", lambda e: e.collective_compute(
                    "AllGather", ALU.bypass, replica_groups=[[0, 1, 2, 3], [4, 5, 6, 7]],
                    ins=[agA_in.ap().opt()], outs=[agA_out.ap().opt()]), r=["ag1_halo"] + list(ag1_toks), w=["agA_out"], dma="cc")
                S.op("pool", lambda e: e.collective_compute(
                    "AllGather", ALU.bypass, replica_groups=[[0, 1, 2, 3], [4, 5, 6, 7]],
                    ins=[agB_in.ap().opt()], outs=[agB_out.ap().opt()]), r=list(kvc_toks), w=["agB_out"], dma="cc")
                del ag1_toks[:]
                del kvc_toks[:]
                S.dma(lambda e: e.dma_start(out=hal1[:], in_=agA_out.ap()[:, 2048:2052].rearrange("(r p) c -> p r c", p=128)),
                      r=["agA_out"], w=["hal1"], q="sp")
                for side, (col, which) in enumerate(((258, 1), (2307, 0))):
                    acc = small[:, 24 + 2 * side:26 + 2 * side]
                    for r in range(4):
                        hv = hal1[:, r, :].rearrange("p (c t) -> p c t", c=2)[:, :, which]
                        if r == 0:
                            S.dve(lambda e, hv=hv, acc=acc, side=side: e.tensor_scalar(
                                out=acc, in0=hv, scalar1=sel[:, side * 4:side * 4 + 1], scalar2=None, op0=ALU.mult),
                                r=["hal1", "sel", "small"], w=["small"])
                        else:
                            S.dve(lambda e, hv=hv, acc=acc, side=side, r=r: e.scalar_tensor_tensor(
                                out=acc, in0=hv, scalar=sel[:, side * 4 + r:side * 4 + r + 1], in1=acc,
                                op0=ALU.mult, op1=ALU.add), r=["hal1", "sel", "small"], w=["small"])
                    S.dve(lambda e, acc=acc, col=col: e.tensor_copy(out=PT[:, :, col], in_=acc), r=["small"], w=["PT"])
                for (c0, n, isctx) in blocks:
                    pc0 = (1 + c0) if isctx else (259 + c0 - NCTX)
                    for m in range(2):
                        t = tmp[m][:, 0:n]
                        wc = lambda tap, m=m: vecT[:, l * VL + O_CCW + tap * 2 + m:l * VL + O_CCW + tap * 2 + m + 1]
                        S.dve(lambda e, t=t, m=m, pc0=pc0, n=n, wc=wc: e.tensor_scalar(
                            out=t, in0=PT[:, m, pc0:pc0 + n], scalar1=wc(1), scalar2=None, op0=ALU.mult),
                            r=["PT", "vecT"], w=["tmp%d" % m])
                        S.dve(lambda e, t=t, m=m, pc0=pc0, n=n, wc=wc: e.scalar_tensor_tensor(
                            out=t, in0=PT[:, m, pc0 - 1:pc0 - 1 + n], scalar=wc(0), in1=t, op0=ALU.mult, op1=ALU.add),
                            r=["PT", "vecT", "tmp%d" % m], w=["tmp%d" % m])
                        S.dve(lambda e, t=t, m=m, pc0=pc0, n=n, wc=wc: e.scalar_tensor_tensor(
                            out=t, in0=PT[:, m, pc0 + 1:pc0 + 1 + n], scalar=wc(2), in1=t, op0=ALU.mult, op1=ALU.add),
                            r=["PT", "vecT", "tmp%d" % m], w=["tmp%d" % m])
                        S.dve(lambda e, t=t, m=m, c0=c0, n=n: e.tensor_tensor(
                            out=CBT[:, m, c0:c0 + n], in0=CBT[:, m, c0:c0 + n], in1=t, op=ALU.mult),
                            r=["CBT", "tmp%d" % m], w=["CBT"])
            stage[0] = "exch1"
            add_step([], exch1)
            stage[0] = "attn"

            for g in range(2):
                def load_kv(slot, stok, g=g):
                    for half in range(2):
                        S.dma(lambda e, half=half: e.dma_start(out=KT[half * 64:(half + 1) * 64, 0:NCTX],
                                                               in_=kctx.ap()[g * 64:(g + 1) * 64, :]),
                              r=["kctx%d" % g], w=[KTOK[half * 5], "tab"], q="sp")
                        for r in range(4):
                            S.dma(lambda e, half=half, r=r: e.dma_start(
                                out=KT[half * 64:(half + 1) * 64, NCTX + r * 2048:NCTX + (r + 1) * 2048],
                                in_=agA_out.ap()[r * 128 + g * 64:r * 128 + (g + 1) * 64, 0:2048]),
                                r=["agA_out"], w=[KTOK[half * 5 + 1 + r], "tab"], q="sp" if r % 2 == 0 else "act")
                    VAv = VA[:, 64:64 + 66 * 128].rearrange("p (t x) -> p t x", x=128)
                    S.dma(lambda e: e.dma_start(out=VAv[:, 0:2, 0:64], in_=vctx.ap()[:, :, g * 64:(g + 1) * 64]),
                          r=["vctx0", "vctx1"], w=[VTOK[4]], q="sp")
                    for r in range(4):
                        S.dma(lambda e, r=r: e.dma_start(
                            out=VAv[:, 2 + r * 16:2 + (r + 1) * 16, 0:64],
                            in_=agB_out.ap()[r * 128:(r + 1) * 128, 0:2048].rearrange("p (t c) -> p t c", c=128)[:, :, g * 64:(g + 1) * 64]),
                            r=["agB_out"], w=[VTOK[r]], q="act" if r % 2 == 0 else "sp")
                add_step([], load_kv)

                for bi, (c0, n, isctx) in enumerate(blocks):
                    if isctx and last:
                        continue
                    j = 1 if isctx else 0
                    tiles = [0, 1] if isctx else list(range(66))
                    kbase = 4 * g

                    def attn(slot, stok, c0=c0, n=n, tiles=tiles, g=g):
                        for pp in range(2):
                            m = 2 * g + pp
                            for ti, t in enumerate(tiles):
                                sb_ = (ti % 2) * 2
                                ptile = ptl[ti % 2]
                                for hh in range(2):
                                    S.pe(lambda e, hh=hh, t=t, sb_=sb_, m=m: e.matmul(
                                        PS(sb_ + hh, n), lhsT=KT[hh * 64:(hh + 1) * 64, t * 128:(t + 1) * 128],
                                        rhs=QT[hh * 64:(hh + 1) * 64, m, c0:c0 + n], start=True, stop=True),
                                        r=KTOK + ["QT"], w=["ps%d" % (sb_ + hh)])
                                if n == 512:
                                    S.act(lambda e, sb_=sb_, ptile=ptile: e.activation(
                                        out=ptile[:, 0:1024], in_=ps[:, sb_ * 512:sb_ * 512 + 1024], func=AF.Exp, scale=0.125),
                                        r=["ps%d" % sb_, "ps%d" % (sb_ + 1)], w=["pt%d_0" % (ti % 2), "pt%d_1" % (ti % 2)])
                                else:
                                    for hh in range(2):
                                        S.act(lambda e, hh=hh, sb_=sb_, ptile=ptile: e.activation(
                                            out=ptile[:, hh * 512:hh * 512 + n], in_=PS(sb_ + hh, n), func=AF.Exp, scale=0.125),
                                            r=["ps%d" % (sb_ + hh)], w=["pt%d_%d" % (ti % 2, hh)])
                                S.pe(lambda e, t=t, ti=ti, ptile=ptile: e.matmul(
                                    PS(4, n), lhsT=VA[:, (2 * t + 1) * 64:(2 * t + 3) * 64], rhs=ptile[:, 0:n],
                                    start=(ti == 0), stop=(ti == len(tiles) - 1)),
                                    r=VTOK + ["pt%d_0" % (ti % 2)], w=["ps4"])
                                S.pe(lambda e, t=t, ti=ti, ptile=ptile: e.matmul(
                                    PS(5, n), lhsT=VA[:, (2 * t) * 64:(2 * t + 2) * 64], rhs=ptile[:, 512:512 + n],
                                    start=(ti == 0), stop=(ti == len(tiles) - 1)),
                                    r=VTOK + ["pt%d_1" % (ti % 2)], w=["ps5"])
                            R = tmp[0]
                            Rs = tmp[1]
                            S.act(lambda e: e.copy(out=Rs[0:64, 0:n], in_=ps[0:64, 5 * 512:5 * 512 + n]), r=["ps5"], w=["tmp1"])
                            S.act(lambda e: e.copy(out=Rs[64:128, 0:n], in_=ps[64:128, 4 * 512:4 * 512 + n]), r=["ps4"], w=["tmp1"])
                            S.pe(lambda e: e.matmul(PS(7, n), lhsT=swapF[:], rhs=Rs[:, 0:n], start=True, stop=True),
                                 r=["tmp1", "swapF"], w=["ps7"])
                            S.dve(lambda e: e.reciprocal(out=R[:, 0:n], in_=PS(7, n)), r=["ps7"], w=["tmp0"])
                            S.dve(lambda e, pp=pp: e.tensor_tensor(out=hT[0:64, pp, 0:n], in0=ps[0:64, 4 * 512:4 * 512 + n],
                                                                   in1=R[0:64, 0:n], op=ALU.mult),
                                  r=["ps4", "tmp0"], w=["h"])
                            S.dve(lambda e, pp=pp: e.tensor_tensor(out=hT[64:128, pp, 0:n], in0=ps[64:128, 5 * 512:5 * 512 + n],
                                                                   in1=R[64:128, 0:n], op=ALU.mult),
                                  r=["ps5", "tmp0"], w=["h"])
                    add_step([], attn)

                    for oc in range(2):
                        def f_wo(slot, stok, oc=oc, c0=c0, n=n, g=g, j=j):
                            sv = slot[:, 0:4 * 512].rearrange("p (k n) -> p k n", k=4)
                            if g == 0:
                                srcs = [aT[:, 0, c0:c0 + n], aT[:, 1, c0:c0 + n], hT[:, 0, 0:n], hT[:, 1, 0:n]]
                            else:
                                srcs = [hT[:, 0, 0:n], hT[:, 1, 0:n], CBT[:, 0, c0:c0 + n], CBT[:, 1, c0:c0 + n]]
                            for mo in range(4):
                                m = oc * 4 + mo
                                psb = 6 + (mo % 2)
                                for kk in range(4):
                                    S.pe(lambda e, kk=kk, mo=mo, psb=psb: e.matmul(
                                        PS(psb, n), lhsT=sv[:, kk, mo * 128:(mo + 1) * 128], rhs=srcs[kk],
                                        start=(kk == 0), stop=(kk == 3)), r=[stok, "h", "aT", "CBT"], w=["ps%d" % psb])
                                S.dve(lambda e, m=m, psb=psb: e.scalar_tensor_tensor(
                                    out=xT[:, m, c0:c0 + n], in0=PS(psb, n), scalar=modT[:, l, 2 * 8 + m, j:j + 1],
                                    in1=xT[:, m, c0:c0 + n], op0=ALU.mult, op1=ALU.add),
                                    r=["ps%d" % psb, "x", "modT"], w=["x"])
                        src = woutv[:, 4 * g:4 * g + 4, oc * 512:(oc + 1) * 512]
                        add_step([(src, lambda slot: slot[:, 0:4 * 512].rearrange("p (k n) -> p k n", k=4))], f_wo)

            def ffn_pre(slot, stok):
                S.pool(lambda e: e.memset(small[:, 9:10], 0.0), r=UTOK, w=UTOK + ["small"])
                S.dve(lambda e: e.tensor_copy(out=xh[:, :, 0], in_=xT[:, :, NCTX]), r=["x"], w=["xh"])
                S.dve(lambda e: e.tensor_copy(out=xh[:, :, 1], in_=xT[:, :, NT - 1]), r=["x"], w=["xh"])
                rstd = tmp[0][:, 0:2]
                rms_stats(lambda k: xh[:, k, :], 2, rstd, ["xh"], "tmp0")
                for k in range(KC):
                    tt = tmp[1 + (k % 2)]
                    S.dve(lambda e, k=k, tt=tt: e.scalar_tensor_tensor(
                        out=tt[:, 0:2], in0=xh[:, k, :], scalar=A2[:, l, k, 0:1], in1=rstd, op0=ALU.mult, op1=ALU.mult),
                        r=["xh", "tmp0", "A"], w=["tmp%d" % (1 + k % 2)])
                    S.act(lambda e, k=k, tt=tt: e.activation(out=h2h[:, k, :], in_=tt[:, 0:2], func=AF.Identity,
                                                             bias=modT[:, l, 3 * 8 + k, 0:1], scale=1.0),
                          r=["tmp%d" % (1 + k % 2), "modT"], w=["h2h"])
                S.dma(lambda e: e.dma_start(out=ag2_in.ap().rearrange("p (k t) -> p k t", k=KC), in_=h2h[:]),
                      r=["h2h"], w=["ag2_in"], q="sp")
                S.op("pool", lambda e: e.collective_compute(
                    "AllGather", ALU.bypass, replica_groups=[[0, 1, 2, 3], [4, 5, 6, 7]],
                    ins=[ag2_in.ap().opt()], outs=[ag2_out.ap().opt()]), r=["ag2_in"], w=["ag2_out"], dma="cc")
                S.dma(lambda e: e.dma_start(out=hal2[:], in_=ag2_out.ap().rearrange("(r p) c -> p r c", p=128)),
                      r=["ag2_out"], w=["hal2"], q="sp")
                for side, which in enumerate((1, 0)):
                    acc = tmp[3][:, side * 8:side * 8 + 8]
                    for r in range(4):
                        hv = hal2[:, r, :].rearrange("p (k t) -> p k t", k=KC)[:, :, which]
                        if r == 0:
                            S.dve(lambda e, hv=hv, acc=acc, side=side: e.tensor_scalar(
                                out=acc, in0=hv, scalar1=sel[:, side * 4:side * 4 + 1], scalar2=None, op0=ALU.mult),
                                r=["hal2", "sel"], w=["tmp3"])
                        else:
                            S.dve(lambda e, hv=hv, acc=acc, side=side, r=r: e.scalar_tensor_tensor(
                                out=acc, in0=hv, scalar=sel[:, side * 4 + r:side * 4 + r + 1], in1=acc,
                                op0=ALU.mult, op1=ALU.add), r=["hal2", "sel", "tmp3"], w=["tmp3"])
                    S.dve(lambda e, acc=acc, side=side: e.tensor_copy(out=h2h[:, :, side], in_=acc), r=["tmp3"], w=["h2h"])
            stage[0] = "ffn"
            add_step([], ffn_pre)

            nlat_blocks = [b for b in blocks if not b[2]]
            for bi, (c0, n, isctx) in enumerate(blocks):
                if isctx and last:
                    continue
                j = 1 if isctx else 0
                first = isctx or (c0 == NCTX)
                lastb = isctx or (c0 + n == NT)
                halo_here = (not isctx) and first
                par = bi % 2

                def f_norm2(slot, stok, c0=c0, n=n, j=j):
                    norm_mod(l, c0, n, j, A2, 3, lambda k: hT[:, k, 0:n], "h")
                add_step([], f_norm2)

                for jj in range(NJ):
                    def f_up(slot, stok, jj=jj, c0=c0, n=n, isctx=isctx, first=first, lastb=lastb,
                             halo_here=halo_here, par=par):
                        sv = slot[:, 0:2048].rearrange("p (k s n) -> p k s n", k=KC, s=2)
                        cts = []
                        for s in range(2):
                            ch = s * NJ + jj
                            psb = 1 + 2 * (jj % 2) + s
                            for k in range(KC):
                                S.pe(lambda e, k=k, s=s, psb=psb: e.matmul(PS(psb, n), lhsT=sv[:, k, s, :], rhs=hT[:, k, 0:n],
                                                                           start=(k == 0), stop=(k == KC - 1)),
                                     r=[stok, "h"], w=["ps%d" % psb])
                            if halo_here:
                                for k in range(KC):
                                    S.pe(lambda e, k=k, s=s, ch=ch: e.matmul(
                                        ps[:, 3584 + 128 + ch * 2:3584 + 128 + ch * 2 + 2], lhsT=sv[:, k, s, :], rhs=h2h[:, k, :],
                                        start=(k == 0), stop=(k == KC - 1)), r=[stok, "h2h"], w=["ps7h"])
                                S.act(lambda e, ch=ch: e.copy(out=uph[:, ch, :], in_=ps[:, 3584 + 128 + ch * 2:3584 + 128 + ch * 2 + 2]),
                                      r=["ps7h"], w=["uph"])
                                S.act(lambda e, ch=ch, par=par: e.copy(out=saves[:, par, ch, 1:2], in_=uph[:, ch, 0:1]),
                                      r=["uph"], w=["saves%d" % par])
                            if isctx:
                                S.pool(lambda e, ch=ch, par=par: e.memset(saves[:, par, ch, :], 0.0), w=["saves%d" % par])
                            wq = lambda tap, ch=ch: vecT[:, l * VL + O_FCW + tap * 44 + ch:l * VL + O_FCW + tap * 44 + ch + 1]
                            up = PS(psb, n)
                            c = tmp[(jj % 2) * 2 + s]
                            ctok = "tmp%d" % ((jj % 2) * 2 + s)
                            sv_ = saves[:, par, ch, :]
                            S.act(lambda e, c=c, up=up, wq=wq: e.activation(out=c[:, 2:n], in_=up[:, 0:n - 2], func=AF.Identity, scale=wq(0)),
                                  r=["ps%d" % psb, "vecT"], w=[ctok])
                            S.pool(lambda e, c=c, sv_=sv_, wq=wq: e.tensor_scalar(out=c[:, 0:2], in0=sv_, scalar1=wq(0), scalar2=None,
                                                                                  op0=ALU.mult), r=["saves%d" % par, "vecT"], w=[ctok])
                            S.dve(lambda e, c=c, sv_=sv_, wq=wq: e.scalar_tensor_tensor(
                                out=c[:, 0:1], in0=sv_[:, 1:2], scalar=wq(1), in1=c[:, 0:1], op0=ALU.mult, op1=ALU.add),
                                r=["saves%d" % par, "vecT", ctok], w=[ctok])
                            S.dve(lambda e, c=c, up=up, wq=wq: e.scalar_tensor_tensor(
                                out=c[:, 1:n], in0=up[:, 0:n - 1], scalar=wq(1), in1=c[:, 1:n], op0=ALU.mult, op1=ALU.add),
                                r=["ps%d" % psb, "vecT", ctok], w=[ctok])
                            S.dve(lambda e, c=c, up=up, wq=wq: e.scalar_tensor_tensor(
                                out=c[:, 0:n], in0=up[:, 0:n], scalar=wq(2), in1=c[:, 0:n], op0=ALU.mult, op1=ALU.add),
                                r=["ps%d" % psb, "vecT", ctok], w=[ctok])
                            if not lastb:
                                S.act(lambda e, up=up, ch=ch, par=par: e.copy(out=saves[:, 1 - par, ch, :], in_=up[:, n - 2:n]),
                                      r=["ps%d" % psb], w=["saves%d" % (1 - par)])
                            else:
                                S.act(lambda e, c=c, up=up, wq=wq: e.activation(out=c[:, n:n + 1], in_=up[:, n - 2:n - 1], func=AF.Identity, scale=wq(0)),
                                      r=["ps%d" % psb, "vecT"], w=[ctok])
                                S.dve(lambda e, c=c, up=up, wq=wq: e.scalar_tensor_tensor(
                                    out=c[:, n:n + 1], in0=up[:, n - 1:n], scalar=wq(1), in1=c[:, n:n + 1], op0=ALU.mult, op1=ALU.add),
                                    r=["ps%d" % psb, "vecT", ctok], w=[ctok])
                                if not isctx:
                                    S.dve(lambda e, c=c, ch=ch, wq=wq: e.scalar_tensor_tensor(
                                        out=c[:, n:n + 1], in0=uph[:, ch, 1:2], scalar=wq(2), in1=c[:, n:n + 1], op0=ALU.mult, op1=ALU.add),
                                        r=["uph", "vecT", ctok], w=[ctok])
                            cts.append((c, ctok))
                        nn = n + 1 if lastb else n
                        (ca, ta), (cg, tg) = cts
                        S.act(lambda e, cg=cg: e.activation(out=cg[:, 0:nn], in_=cg[:, 0:nn], func=AF.Silu), r=[tg], w=[tg])
                        S.pool(lambda e, ca=ca, cg=cg, jj=jj: e.tensor_tensor(out=actT[:, jj, 0:nn], in0=ca[:, 0:nn], in1=cg[:, 0:nn], op=ALU.mult),
                               r=[ta, tg], w=["actT"])
                    srcs = [(upv[:, :, jj * 128:(jj + 1) * 128],
                             lambda slot: slot[:, 0:2048].rearrange("p (k s n) -> p k s n", k=KC, s=2)[:, :, 0, :]),
                            (upv[:, :, DFF + jj * 128:DFF + (jj + 1) * 128],
                             lambda slot: slot[:, 0:2048].rearrange("p (k s n) -> p k s n", k=KC, s=2)[:, :, 1, :])]
                    add_step(srcs, f_up)

                for m in range(KC):
                    def f_dn(slot, stok, m=m, c0=c0, n=n, j=j, first=first, lastb=lastb):
                        sv = slot[:, 0:NJ * 128].rearrange("p (j n) -> p j n", j=NJ)
                        psb = 5 + (m % 2)
                        for jj in range(NJ):
                            S.pe(lambda e, jj=jj, psb=psb: e.matmul(PS(psb, n), lhsT=sv[:, jj, :], rhs=actT[:, jj, 0:n],
                                                                    start=(jj == 0), stop=(jj == NJ - 1)),
                                 r=[stok, "actT"], w=["ps%d" % psb])
                        if lastb:
                            for jj in range(NJ):
                                S.pe(lambda e, jj=jj, m=m: e.matmul(ps[:, 3584 + 400 + m:3584 + 400 + m + 1], lhsT=sv[:, jj, :],
                                                                    rhs=actT[:, jj, n:n + 1], start=(jj == 0), stop=(jj == NJ - 1)),
                                     r=[stok, "actT"], w=["ps7f"])
                        lo = 1 if first else 0
                        gate = modT[:, l, 5 * 8 + m, j:j + 1]
                        S.dve(lambda e, m=m, psb=psb, lo=lo, gate=gate: e.scalar_tensor_tensor(
                            out=xT[:, m, c0 - 1 + lo:c0 - 1 + n], in0=ps[:, psb * 512 + lo:psb * 512 + n], scalar=gate,
                            in1=xT[:, m, c0 - 1 + lo:c0 - 1 + n], op0=ALU.mult, op1=ALU.add),
                            r=["ps%d" % psb, "x", "modT"], w=["x"])
                        if lastb:
                            S.dve(lambda e, m=m, gate=gate: e.scalar_tensor_tensor(
                                out=xT[:, m, c0 + n - 1:c0 + n], in0=ps[:, 3584 + 400 + m:3584 + 400 + m + 1], scalar=gate,
                                in1=xT[:, m, c0 + n - 1:c0 + n], op0=ALU.mult, op1=ALU.add),
                                r=["ps7f", "x", "modT"], w=["x"])
                    add_step([(dnv[:, :, m * 128:(m + 1) * 128],
                               lambda slot: slot[:, 0:NJ * 128].rearrange("p (j n) -> p j n", j=NJ))], f_dn)

        for l in range(depth):
            layer(l)

        def final(slot, stok):
            for bi, (c0, n, isctx) in enumerate(blocks):
                if isctx:
                    continue
                rstd = tmp[0][:, 0:n]
                rms_stats(lambda k, c0=c0, n=n: xT[:, k, c0:c0 + n], n, rstd, ["x"], "tmp0")
                for k in range(KC):
                    tt = tmp[1 + (k % 4)]
                    S.dve(lambda e, k=k, tt=tt, c0=c0, n=n, rstd=rstd: e.scalar_tensor_tensor(
                        out=tt[:, 0:n], in0=xT[:, k, c0:c0 + n], scalar=vecT[:, O_FG + k:O_FG + k + 1], in1=rstd,
                        op0=ALU.mult, op1=ALU.mult), r=["x", "tmp0", "vecT"], w=["tmp%d" % (1 + k % 4)])
                    S.dma(lambda e, k=k, tt=tt, c0=c0, n=n: e.dma_start(out=yT[:, k, c0 - NCTX:c0 - NCTX + n], in_=tt[:, 0:n]),
                          r=["tmp%d" % (1 + k % 4)], w=["yT"], q="sp" if k % 2 == 0 else "act")
        stage[0] = "final"
        add_step([], final)
        if upto is not None:
            order = ["pro", "m1", "exch1", "attn", "ffn", "final"]
            lim = order.index(upto)
            steps[:] = [st for st in steps if order.index(st[2]) <= lim or st[2] == "final"]

        def emit_loads(i):
            loads = steps[i][0]
            slot = wbuf[i % 2]
            for (src, viewfn) in loads:
                S.dma(lambda e, src=src, dst=viewfn(slot): e.dma_start(out=dst, in_=src), w=["wb%d" % (i % 2)], q="pool")

        emit_loads(0)
        for i in range(len(steps)):
            if i + 1 < len(steps):
                emit_loads(i + 1)
            steps[i][1](wbuf[i % 2], "wb%d" % (i % 2))
        for nm in dbg_out:
            pass
        S.op("sp", lambda e: None, r=["yT"] + ["dbg_" + nm for nm in dbg_out])
        S.emit(sems)
    return nc


_NC_CACHE = {}


def _consts():
    c = np.zeros((128, 3, 128), np.float32)
    c[:, 0, :] = 1.0
    c[0:64, 1, 0:64] = 1.0
    c[64:128, 1, 64:128] = 1.0
    for o in (0, 64):
        for a in (0, 1):
            base = o + 32 * a
            for i in range(16):
                c[base + 16 + i, 2, base + i] = -1.0
                c[base + i, 2, base + 16 + i] = 1.0
    return c


def _fm(a):
    n = a.shape[0]
    return np.ascontiguousarray(a.reshape(n, KC, 128).transpose(2, 1, 0))


def _vec(v):
    v = np.asarray(v, np.float32)
    return v.reshape(-1, 128).T


def prepare_inputs(x, c, ctx, c_ctx, w_mod, b_mod, norm1_g, w_in, q_norm_g, k_norm_g, gmlp_w, gmlp_b,
                   conv_c_w, w_out, norm2_g, ffn_up, ffn_conv_w, ffn_down, final_g, depth=DEPTH):
    f = lambda a: np.ascontiguousarray(np.asarray(a, dtype=np.float32))
    x, c, ctx, c_ctx = f(x), f(c), f(ctx), f(c_ctx)
    vecT = np.zeros((128, NV), np.float32)
    for l in range(DEPTH):
        b = l * VL
        vecT[:, b + O_N1:b + O_N1 + 8] = _vec(norm1_g[l])
        vecT[:, b + O_N2:b + O_N2 + 8] = _vec(norm2_g[l])
        vecT[:, b + O_BM:b + O_BM + 96] = np.repeat(_vec(b_mod[l]), 2, axis=1)
        vecT[:, b + O_FCW:b + O_FCW + 132] = _vec(ffn_conv_w[l])
        vecT[:, b + O_CCW:b + O_CCW + 6] = _vec(conv_c_w[l])
        vecT[:, b + O_QG] = np.tile(np.asarray(q_norm_g[l], np.float32), 2)
        vecT[:, b + O_KG] = np.tile(np.asarray(k_norm_g[l], np.float32), 2)
    vecT[:, O_FG:O_FG + 8] = _vec(final_g)
    vecT[:, O_FIDX] = np.arange(128) % 16
    wsT = np.ascontiguousarray(np.asarray(gmlp_w, np.float32).transpose(0, 3, 1, 2))
    gb = np.asarray(gmlp_b, np.float32).reshape(DEPTH, 2, 2, 128)
    gbias = np.ascontiguousarray(np.repeat(gb.transpose(0, 2, 1, 3), 64, axis=1))
    consts = _consts()
    swapm = np.zeros((128, 128), np.float32)
    swapm[np.arange(128), (np.arange(128) + 64) % 128] = 1.0
    dd = depth
    shared = dict(vecT=vecT, consts=consts, swapm=swapm, wsT=wsT[:dd], gbias=gbias[:dd], w_mod=f(w_mod)[:dd], w_in=f(w_in)[:dd],
                  w_out=f(w_out)[:dd], ffn_up=f(ffn_up)[:dd], ffn_down=f(ffn_down)[:dd])
    in_maps = []
    for core in range(8):
        b, q = core // 4, core % 4
        t0 = q * NLAT
        tok = np.arange(t0, t0 + NLAT)
        rowcol = np.stack([tok // 64, tok % 64]).astype(np.float32)
        sel = np.zeros((128, 8), np.float32)
        if q > 0:
            sel[:, q - 1] = 1.0
        if q < 3:
            sel[:, 4 + q + 1] = 1.0
        cv = np.stack([c[b], c_ctx], axis=1)
        cvT = np.ascontiguousarray(cv.reshape(KC, 128, 2).transpose(1, 0, 2))
        m = dict(shared)
        m.update(xT_in=_fm(x[b, t0:t0 + NLAT]), cT_in=_fm(ctx[b]), cvT_in=cvT, rowcol=rowcol, sel=sel)
        in_maps.append(m)
    return in_maps


def kernel(**inputs):
    in_maps = prepare_inputs(**inputs)
    if "nc" not in _NC_CACHE:
        _NC_CACHE["nc"] = build_nc()
    res = run_bass_kernel_spmd(_NC_CACHE["nc"], in_maps, core_ids=list(range(8)))
    out = np.zeros((2, 4 * NLAT, D), np.float32)
    for core in range(8):
        b, q = core // 4, core % 4
        yT = np.asarray(res.results[core]["yT"], np.float32)
        out[b, q * NLAT:(q + 1) * NLAT, :] = yT.transpose(2, 1, 0).reshape(NLAT, D)
    return out
```
